# Optimizing a Trainium2 kernel written in Bass

```python
import jax, jax.numpy as jnp
from jax import lax
import numpy as np

D_MODEL = 2048
BATCH = 8
SEQ = 2048
DEPTH = 1

CTX_LEN = 256
GRID_W = 64
RET_HEADS = 8
RET_DK = 256
RET_DV = 256
RET_QK = RET_HEADS * RET_DK
RET_VW = RET_HEADS * RET_DV
RET_CHUNK = 128
ROPE_FREQS = RET_DK // 4
ROPE_BASE = 10000.0
SG_GROUPS = 8
SG_CHUNK = 128
SG_WIDTH = 2048
SG_GD = SG_WIDTH // SG_GROUPS
FFN_HIDDEN = -(-8 * D_MODEL // (3 * 256)) * 256
EPS = 1e-6
Q_OFF = 0
K_OFF = Q_OFF + RET_QK
V_OFF = K_OFF + RET_QK
G_OFF = V_OFF + RET_VW
U_OFF = G_OFF + RET_VW
VS_OFF = U_OFF + SG_WIDTH
GR_OFF = VS_OFF + SG_WIDTH
GS_OFF = GR_OFF + D_MODEL
D_IN = GS_OFF + D_MODEL

kernel_name = "hybrid_retention_gmlp_prefix_dit"

F32 = jnp.float32


def rmsnorm(x, g):
    xf = x.astype(F32)
    y = xf * lax.rsqrt(jnp.mean(xf * xf, axis=-1, keepdims=True) + EPS)
    return (y * g.astype(F32)).astype(x.dtype)


def layernorm(x, g, b):
    xf = x.astype(F32)
    mu = jnp.mean(xf, axis=-1, keepdims=True)
    var = jnp.mean(jnp.square(xf - mu), axis=-1, keepdims=True)
    y = (xf - mu) * lax.rsqrt(var + EPS)
    return (y * g.astype(F32) + b.astype(F32)).astype(x.dtype)


def adaln(cond, w_mod, b_mod):
    return jnp.split(jax.nn.silu(cond) @ w_mod + b_mod, 6, axis=-1)


def modulate(h, shift, scale):
    return h * (1.0 + scale) + shift


def rope_tables(L):
    rows = L // GRID_W
    row = jnp.repeat(jnp.arange(rows), GRID_W)
    col = jnp.tile(jnp.arange(GRID_W), rows)
    freq = ROPE_BASE ** (-jnp.arange(ROPE_FREQS, dtype=F32) / ROPE_FREQS)
    ang = jnp.stack([row, col], axis=-1).astype(F32)[:, :, None] * freq
    return jnp.cos(ang), jnp.sin(ang)


def apply_rope(x, cos, sin):
    B, L, H, Dk = x.shape
    xb = x.reshape(B, L, H, 2, 2, ROPE_FREQS)
    x1, x2 = xb[..., 0, :], xb[..., 1, :]
    c = cos[None, :, None]
    s = sin[None, :, None]
    return jnp.stack([x1 * c - x2 * s, x2 * c + x1 * s], axis=-2).reshape(B, L, H, Dk).astype(x.dtype)


def log_decay(logit):
    return -jax.nn.softplus(-logit.astype(F32))


def retention_scan(q, k, v, lg, s0, include_diag):
    B, H, L, _ = q.shape
    Dv = v.shape[-1]
    C = RET_CHUNK
    n = L // C

    def chunks(t):
        return jnp.moveaxis(t.reshape(B, H, n, C, t.shape[-1]), 2, 0)

    idx = jnp.arange(C, dtype=F32)
    diff = idx[:, None] - idx[None, :]
    mask = (diff >= 0) if include_diag else (diff > 0)
    decay_in = jnp.where(mask, jnp.exp(lg[:, None, None] * jnp.maximum(diff, 0.0)), 0.0)
    q_dec = jnp.exp(lg[:, None] * (idx + 1.0))[None, :, :, None]
    k_dec = jnp.exp(lg[:, None] * (C - 1.0 - idx))[None, :, :, None]
    c_dec = jnp.exp(lg * C)[None, :, None, None]

    def step(S, qkv):
        qc, kc, vc = qkv
        scores = jnp.einsum('bhid,bhjd->bhij', qc, kc) * decay_in
        out = (jnp.einsum('bhij,bhje->bhie', scores, vc)
               + jnp.einsum('bhid,bhde->bhie', qc, S) * q_dec)
        S = S * c_dec + jnp.einsum('bhjd,bhje->bhde', kc * k_dec, vc)
        return S, out

    _, out = lax.scan(step, s0, (chunks(q), chunks(k), chunks(v)))
    return jnp.moveaxis(out, 0, 2).reshape(B, H, L, Dv)


def bidirectional_retention(q, k, v, lg_f, lg_b, s_f, s_b):
    q, k, v = (jnp.swapaxes(t.astype(F32), 1, 2) for t in (q, k, v))
    flip = lambda t: jnp.flip(t, axis=2)
    fwd = retention_scan(q, k, v, lg_f, s_f, True)
    bwd = flip(retention_scan(flip(q), flip(k), flip(v), lg_b, s_b, False))
    return jnp.swapaxes(fwd + bwd, 1, 2)


def context_states(k, v, lg_f, lg_b):
    Lc = k.shape[2]
    j = jnp.arange(Lc, dtype=F32)
    w_f = jnp.exp(lg_f[:, None] * (Lc - 1.0 - j))[None, :, :, None]
    w_b = jnp.exp(lg_b[:, None] * j)[None, :, :, None]
    s_f = jnp.einsum('bhld,bhle->bhde', k * w_f, v)
    s_b = jnp.einsum('bhld,bhle->bhde', k * w_b, v)
    return s_f, s_b


def spatial_gating(u, vs, ln_g, ln_b, w_s, b_s):
    B, L, _ = u.shape
    n = L // SG_CHUNK
    vn = layernorm(vs, ln_g, ln_b).reshape(B, n, SG_CHUNK, SG_GROUPS, SG_GD)
    mixed = jnp.einsum('gij,bnjgd->bnigd', w_s, vn) + b_s.T[None, None, :, :, None]
    return u * mixed.reshape(B, L, SG_WIDTH)


def token_mixer(h, s_f, s_b, rope, w_in, lg_f, lg_b, sg_ln_g, sg_ln_b, sg_w, sg_b,
                w_ret_o, w_sg_o, w_out):
    B, L, _ = h.shape
    p = h @ w_in
    q = p[..., Q_OFF:K_OFF].reshape(B, L, RET_HEADS, RET_DK)
    k = p[..., K_OFF:V_OFF].reshape(B, L, RET_HEADS, RET_DK) * (RET_DK ** -0.5)
    v = p[..., V_OFF:G_OFF].reshape(B, L, RET_HEADS, RET_DV)
    g_ret = p[..., G_OFF:U_OFF]
    u = jax.nn.gelu(p[..., U_OFF:VS_OFF])
    vs = jax.nn.gelu(p[..., VS_OFF:GR_OFF])
    gate_r = jax.nn.sigmoid(p[..., GR_OFF:GS_OFF].astype(F32))
    gate_s = jax.nn.sigmoid(p[..., GS_OFF:D_IN].astype(F32))
    if rope is not None:
        q = apply_rope(q, *rope)
        k = apply_rope(k, *rope)
    ret = bidirectional_retention(q, k, v, lg_f, lg_b, s_f, s_b)
    ret = ret * lax.rsqrt(jnp.mean(ret * ret, axis=-1, keepdims=True) + EPS)
    ret = ret.reshape(B, L, RET_VW) * jax.nn.silu(g_ret)
    y_ret = ret @ w_ret_o
    y_sg = spatial_gating(u, vs, sg_ln_g, sg_ln_b, sg_w, sg_b) @ w_sg_o
    return (gate_r * y_ret + gate_s * y_sg) @ w_out


def swiglu(h, w_ffn_in, w_ffn_out):
    a, b = jnp.split(h @ w_ffn_in, 2, axis=-1)
    return (jax.nn.silu(a) * b) @ w_ffn_out


def setup_inputs(seed: int = 0) -> dict:
    key = jax.random.key(seed)
    ks = jax.random.split(key, 24)
    nrm = lambda k, shape, s: jax.random.normal(k, shape, F32) * s
    base_logit = jnp.log(2.0 ** (5.0 + jnp.arange(RET_HEADS, dtype=F32)) - 1.0)
    return {
        "x": nrm(ks[0], (BATCH, SEQ, D_MODEL), 1.0),
        "c": nrm(ks[1], (BATCH, D_MODEL), 1.0),
        "ctx": nrm(ks[2], (BATCH, CTX_LEN, D_MODEL), 1.0),
        "c_ctx": nrm(ks[3], (D_MODEL,), 1.0),
        "w_mod": nrm(ks[4], (DEPTH, D_MODEL, 6 * D_MODEL), 0.5 * D_MODEL ** -0.5),
        "b_mod": nrm(ks[5], (DEPTH, 6 * D_MODEL), 0.01),
        "norm1_g": 1.0 + nrm(ks[6], (DEPTH, D_MODEL), 0.02),
        "w_in": nrm(ks[7], (DEPTH, D_MODEL, D_IN), D_MODEL ** -0.5),
        "ret_decay_fwd": base_logit[None] + nrm(ks[8], (DEPTH, RET_HEADS), 0.1),
        "ret_decay_bwd": base_logit[None] + nrm(ks[9], (DEPTH, RET_HEADS), 0.1),
        "sg_ln_g": 1.0 + nrm(ks[10], (DEPTH, SG_WIDTH), 0.02),
        "sg_ln_b": nrm(ks[11], (DEPTH, SG_WIDTH), 0.02),
        "sg_w": nrm(ks[12], (DEPTH, SG_GROUPS, SG_CHUNK, SG_CHUNK), 0.5 * SG_CHUNK ** -0.5),
        "sg_b": 1.0 + nrm(ks[13], (DEPTH, SG_GROUPS, SG_CHUNK), 0.02),
        "w_ret_o": nrm(ks[14], (DEPTH, RET_VW, D_MODEL), RET_VW ** -0.5),
        "w_sg_o": nrm(ks[15], (DEPTH, SG_WIDTH, D_MODEL), SG_WIDTH ** -0.5),
        "w_out": nrm(ks[16], (DEPTH, D_MODEL, D_MODEL), D_MODEL ** -0.5),
        "norm2_g": 1.0 + nrm(ks[17], (DEPTH, D_MODEL), 0.02),
        "w_ffn_in": nrm(ks[18], (DEPTH, D_MODEL, 2 * FFN_HIDDEN), D_MODEL ** -0.5),
        "w_ffn_out": nrm(ks[19], (DEPTH, FFN_HIDDEN, D_MODEL), FFN_HIDDEN ** -0.5),
        "final_norm_g": 1.0 + nrm(ks[20], (D_MODEL,), 0.02),
    }


def reference(x, c, ctx, c_ctx, w_mod, b_mod, norm1_g, w_in, ret_decay_fwd, ret_decay_bwd,
              sg_ln_g, sg_ln_b, sg_w, sg_b, w_ret_o, w_sg_o, w_out, norm2_g,
              w_ffn_in, w_ffn_out, final_norm_g):
    B, L, _ = x.shape
    rope = rope_tables(L)
    ctx_s = ctx
    for l in range(DEPTH):
        sh1, sc1, gt1, sh2, sc2, gt2 = (t[:, None, :] for t in adaln(c, w_mod[l], b_mod[l]))
        csh1, csc1, cgt1, csh2, csc2, cgt2 = adaln(c_ctx, w_mod[l], b_mod[l])
        lg_f = log_decay(ret_decay_fwd[l])
        lg_b = log_decay(ret_decay_bwd[l])
        mixer_params = (w_in[l], lg_f, lg_b, sg_ln_g[l], sg_ln_b[l], sg_w[l], sg_b[l],
                        w_ret_o[l], w_sg_o[l], w_out[l])

        hc = modulate(rmsnorm(ctx_s, norm1_g[l]), csh1, csc1)
        Lc = hc.shape[1]
        kc = (hc @ w_in[l][:, K_OFF:V_OFF]).reshape(B, Lc, RET_HEADS, RET_DK) * (RET_DK ** -0.5)
        vc = (hc @ w_in[l][:, V_OFF:G_OFF]).reshape(B, Lc, RET_HEADS, RET_DV)
        s_f, s_b = context_states(jnp.swapaxes(kc.astype(F32), 1, 2),
                                  jnp.swapaxes(vc.astype(F32), 1, 2), lg_f, lg_b)

        h = modulate(rmsnorm(x, norm1_g[l]), sh1, sc1)
        x = x + gt1 * token_mixer(h, s_f, s_b, rope, *mixer_params)
        h2 = modulate(rmsnorm(x, norm2_g[l]), sh2, sc2)
        x = x + gt2 * swiglu(h2, w_ffn_in[l], w_ffn_out[l])

        if l < DEPTH - 1:
            zero = jnp.zeros((B, RET_HEADS, RET_DK, RET_DV), F32)
            ctx_s = ctx_s + cgt1 * token_mixer(hc, zero, zero, None, *mixer_params)
            hc2 = modulate(rmsnorm(ctx_s, norm2_g[l]), csh2, csc2)
            ctx_s = ctx_s + cgt2 * swiglu(hc2, w_ffn_in[l], w_ffn_out[l])
    return rmsnorm(x, final_norm_g)
```

```python
import math
import numpy as np
import concourse.bass as bass
import concourse.mybir as mybir
from concourse.bass_utils import run_bass_kernel_spmd
from contextlib import ExitStack

F32 = mybir.dt.float32
BF16 = mybir.dt.bfloat16
AF = mybir.ActivationFunctionType
ALU = mybir.AluOpType

D = 2048
L = 2048
LC = 256
NK = 16
H = 8
DIN = 16384
Q_OFF, K_OFF, V_OFF, G_OFF, U_OFF, VS_OFF, GR_OFF, GS_OFF = [i * 2048 for i in range(8)]
FH = 5632
NFC = FH // 128
EPS = 1e-6
TOFF = 2176
TW = 4480
LK = L + LC


class Buf:
    __slots__ = ("name", "w", "r")

    def __init__(self, name):
        self.name = name
        self.w = None
        self.r = {}


class Sched:
    def __init__(self, nc, es, n_dma_sems=8):
        self.nc = nc
        self.E = {}
        self.sems = {}
        for name, e in (("pe", nc.tensor), ("act", nc.scalar), ("dve", nc.vector),
                        ("pool", nc.gpsimd), ("sp", nc.sync)):
            sem = es.enter_context(nc.semaphore(f"s_{name}"))
            self.sems[f"s_{name}"] = sem
            self.E[name] = dict(e=e, sem=f"s_{name}", cnt=0, waited={}, name=name)
        self.dq = {}
        for q in ("sp", "pool"):
            lst = []
            for i in range(n_dma_sems):
                k = f"d_{q}{i}"
                self.sems[k] = es.enter_context(nc.semaphore(k))
                lst.append([k, 0])
            self.dq[q] = dict(sems=lst, i=0)
        self.nwaits = 0
        self.nops = 0

    def _wait(self, E, t):
        k, v, _ = t
        if E["waited"].get(k, 0) >= v:
            return
        E["waited"][k] = v
        E["e"].wait_ge(self.sems[k], v)
        self.nwaits += 1

    def _deps(self, E, reads, writes):
        en = E["name"]
        for b in reads:
            if b.w is not None:
                if not (en == "pe" and b.w[2] == "pe"):
                    self._wait(E, b.w)
        for b in writes:
            if b.w is not None and not (en == "pe" and b.w[2] == "pe"):
                self._wait(E, b.w)
            for k, (v, ren) in b.r.items():
                if not (en == "pe" and ren == "pe"):
                    self._wait(E, (k, v, ren))

    def _record(self, ticket, reads, writes):
        k, v, en = ticket
        for b in reads:
            old = b.r.get(k)
            if old is None or old[0] < v:
                b.r[k] = (v, en)
        for b in writes:
            b.w = ticket
            b.r = {}

    def op(self, en, fn, reads=(), writes=(), inc=True):
        E = self.E[en]
        self._deps(E, reads, writes)
        ins = fn(E["e"])
        ticket = (E["sem"], E["cnt"] + 1, en)
        if inc:
            ins.then_inc(self.sems[E["sem"]], 1)
            E["cnt"] += 1
        self._record(ticket, reads, writes)
        self.nops += 1
        return ins

    def dma(self, q, out, in_, reads=(), writes=(), **kw):
        E = self.E[q]
        for b in reads:
            if b.w is not None:
                self._wait(E, b.w)
        for b in writes:
            if b.w is not None:
                self._wait(E, b.w)
            for k, (v, ren) in b.r.items():
                self._wait(E, (k, v, ren))
        dq = self.dq[q]
        slot = dq["sems"][dq["i"] % len(dq["sems"])]
        dq["i"] += 1
        k = slot[0]
        if slot[1] > 0:
            self._wait(E, (k, slot[1], "dma"))
        slot[1] += 16
        ins = E["e"].dma_start(out=out, in_=in_, **kw)
        ins.then_inc(self.sems[k], 16)
        ticket = (k, slot[1], "dma")
        self._record(ticket, reads, writes)
        return ticket

    def barrier(self):
        for en, E in self.E.items():
            for on, O in self.E.items():
                if on != en and O["cnt"] > 0:
                    self._wait(E, (O["sem"], O["cnt"], on))
            for q in self.dq.values():
                for k, v in q["sems"]:
                    if v > 0:
                        self._wait(E, (k, v, "dma"))

    def finish(self, en, bufs):
        E = self.E[en]
        for b in bufs:
            if b.w is not None:
                self._wait(E, b.w)


class Rot:
    def __init__(self, items):
        self.items = items
        self.i = 0

    def get(self):
        it = self.items[self.i % len(self.items)]
        self.i += 1
        return it


def build_program(stop_after=99):
    nc = bass.Bass("TRN2", target_bir_lowering=False)

    def din(name, shape):
        return nc.dram_tensor(name, list(shape), F32, kind="ExternalInput").ap()

    x = din("x", [L, D]); c_in = din("c", [NK, 128]); ctx = din("ctx", [LC, D]); cctx = din("c_ctx", [NK, 128])
    w_mod = din("w_mod", [D, 6 * D]); b_mod = din("b_mod", [96, 128]); n1g = din("norm1_g", [NK, 128])
    w_in = din("w_in", [D, DIN]); decf = din("ret_decay_fwd", [1, H]); decb = din("ret_decay_bwd", [1, H])
    lng = din("sg_ln_g", [NK, 128]); lnb = din("sg_ln_b", [NK, 128]); sgw = din("sg_w", [H, 128, 128]); sgb = din("sg_b", [1, H * 128])
    w_ret_o = din("w_ret_o", [D, D]); w_sg_o = din("w_sg_o", [D, D]); w_out = din("w_out", [D, D]); n2g = din("norm2_g", [NK, 128])
    w_ffn_in = din("w_ffn_in", [D, 2 * FH]); w_ffn_out = din("w_ffn_out", [FH, D]); fng = din("final_norm_g", [1, D])
    identd = din("k_ident", [128, 128]); ropec = din("k_rope_c", [128, L]); ropes = din("k_rope_s", [128, L]); tdiffd = din("k_tdiff", [128, TW])
    out = nc.dram_tensor("out", [L, D], F32, kind="ExternalOutput").ap()
    dbg = stop_after < 99
    hT_d = nc.dram_tensor("hT_d", [128, NK, L], BF16, kind="ExternalOutput" if dbg else "Internal").ap()
    retT_d = nc.dram_tensor("retT_d", [128, NK, L], BF16, kind="ExternalOutput" if dbg else "Internal").ap()

    with ExitStack() as es:
        S = Sched(nc, es)

        def sb(st, name, shape, dt=F32):
            return st.enter_context(nc.sbuf_tensor(name, list(shape), dt))

        pbanks = []
        for i in range(6):
            t = es.enter_context(nc.psum_tensor(f"pb{i}", [128, 512], F32))
            pbanks.append((t, Buf(f"pb{i}")))
        ptb = []
        for i in range(2):
            t = es.enter_context(nc.psum_tensor(f"ptb{i}", [128, 4, 128], BF16))
            ptb.append((t, Buf(f"ptb{i}")))
        PT = Rot(ptb)

        identf = sb(es, "identf", [128, 128]); identb = sb(es, "identb", [128, 128], BF16); onesb = sb(es, "onesb", [128, 128], BF16)
        cst = sb(es, "cst", [128, 4])
        modp = sb(es, "modp", [128, 96, 2])
        fmA = sb(es, "fmA", [128, 96]); fmB = sb(es, "fmB", [128, 96])
        gm1 = sb(es, "gm1", [128, NK]); cgm1 = sb(es, "cgm1", [128, NK]); gm2 = sb(es, "gm2", [128, NK])
        lgf = sb(es, "lgf", [128, H]); nlgb = sb(es, "nlgb", [128, H])
        wsT = sb(es, "wsT", [128, H, 128], BF16)
        ss = sb(es, "ss", [128, 8]); rstd = sb(es, "rstd", [128, 8])
        sT = sb(es, "sT", [128, NK, 2], BF16); BsT = Buf("sT")
        Bc = Buf("consts"); Bmod = Buf("modp"); Bfm = Buf("fm"); Bg = Buf("gm"); Blg = Buf("lg"); Bws = Buf("wsT"); Brs = Buf("rsb")
        Bss = Buf("ss"); Brstd = Buf("rstd")

        p2 = es.enter_context(ExitStack())
        hT = sb(p2, "hT", [128, NK, LK], BF16); BhT = Buf("hT")
        p1 = ExitStack(); p1.__enter__()
        if True:
            p0 = p1
            PS = Rot(pbanks)
            stA = sb(p0, "stA", [96, 128]); stB = sb(p0, "stB", [96, 128])
            wsf = sb(p0, "wsf", [128, H, 128]); wsb = sb(p0, "wsb", [128, H, 128], BF16)
            dl = sb(p0, "dl", [128, 2 * H]); de = sb(p0, "de", [128, 2 * H]); dp = sb(p0, "dp", [128, 2 * H])
            wms = [(sb(p0, f"wm{i}", [128, NK, 512], BF16), Buf(f"wm{i}")) for i in range(2)]
            WM = Rot(wms)
            BstA = Buf("stA"); BstB = Buf("stB"); Bwsf = Buf("wsf"); Bd = Buf("dl")

            S.dma("sp", identf[:], identd, writes=[Bc])
            S.dma("sp", stA[0:16, :], c_in, writes=[BstA])
            S.dma("sp", stA[16:32, :], cctx, writes=[BstA])
            S.dma("sp", stA[32:48, :], n1g, writes=[BstA])
            S.dma("sp", stA[48:64, :], n2g, writes=[BstA])
            S.dma("sp", stA[64:80, :], lng, writes=[BstA])
            S.dma("sp", stA[80:96, :], lnb, writes=[BstA])
            S.dma("sp", stB[:], b_mod, writes=[BstB])
            S.dma("sp", wsf[:], sgw.rearrange("g i j -> i g j"), writes=[Bwsf])
            S.dma("sp", dl[:, 0:H], decf.partition_broadcast(128), writes=[Bd])
            S.dma("sp", dl[:, H:2 * H], decb.partition_broadcast(128), writes=[Bd])

            S.op("dve", lambda e: e.tensor_copy(out=identb[:], in_=identf[:]), reads=[Bc], writes=[Bc])
            S.op("dve", lambda e: e.memset(onesb[:], 1.0), writes=[Bc])
            S.op("dve", lambda e: e.memset(cst[:, 0:1], EPS), writes=[Bc])
            S.op("dve", lambda e: e.memset(cst[:, 1:2], -math.log(16.0)), writes=[Bc])
            S.op("dve", lambda e: e.memset(cst[:, 2:3], 1.0), writes=[Bc])
            S.op("dve", lambda e: e.memset(cst[:, 3:4], 0.0), writes=[Bc])

            for st_, Bst, fm in ((stA, BstA, fmA), (stB, BstB, fmB)):
                pt_, Bp = PS.get()
                S.op("pe", lambda e: e.transpose(out=pt_[:, 0:96], in_=st_[:], identity=identf[0:96, 0:96]),
                     reads=[Bst, Bc], writes=[Bp])
                S.op("dve", lambda e: e.tensor_copy(out=fm[:], in_=pt_[:, 0:96]), reads=[Bp], writes=[Bfm])
            S.op("act", lambda e: e.activation(out=sT[:, :, 0], in_=fmA[:, 0:16], func=AF.Silu), reads=[Bfm], writes=[BsT])
            S.op("act", lambda e: e.activation(out=sT[:, :, 1], in_=fmA[:, 16:32], func=AF.Silu), reads=[Bfm], writes=[BsT])

            S.op("act", lambda e: e.activation(out=de[:], in_=dl[:], func=AF.Exp, scale=-1.0), reads=[Bd], writes=[Bd])
            S.op("dve", lambda e: e.tensor_scalar(out=dp[:], in0=de[:], scalar1=-1.0 / 6.0, scalar2=1.0 / 5.0, op0=ALU.mult, op1=ALU.add), reads=[Bd], writes=[Bd])
            for coef in (1.0 / 4.0, 1.0 / 3.0, 1.0 / 2.0, 1.0):
                S.op("dve", lambda e: e.tensor_tensor(out=dp[:], in0=dp[:], in1=de[:], op=ALU.mult), reads=[Bd], writes=[Bd])
                S.op("dve", lambda e: e.tensor_scalar(out=dp[:], in0=dp[:], scalar1=-1.0, scalar2=coef, op0=ALU.mult, op1=ALU.add), reads=[Bd], writes=[Bd])
            S.op("dve", lambda e: e.tensor_tensor(out=dp[:], in0=dp[:], in1=de[:], op=ALU.mult), reads=[Bd], writes=[Bd])
            S.op("dve", lambda e: e.tensor_scalar(out=lgf[:], in0=dp[:, 0:H], scalar1=-1.0, scalar2=None, op0=ALU.mult), reads=[Bd], writes=[Blg])
            S.op("dve", lambda e: e.tensor_copy(out=nlgb[:], in_=dp[:, H:2 * H]), reads=[Bd], writes=[Blg])

            S.op("dve", lambda e: e.tensor_copy(out=wsb[:], in_=wsf[:]), reads=[Bwsf], writes=[Bwsf])
            for g2 in range(2):
                pt_, Bp = PT.get()
                for j in range(4):
                    g = g2 * 4 + j
                    S.op("pe", lambda e: e.transpose(out=pt_[:, j, :], in_=wsb[:, g, :], identity=identb[:]),
                         reads=[Bwsf, Bc], writes=[Bp], inc=(j == 3))
                S.op("dve", lambda e: e.tensor_copy(out=wsT[:, g2 * 4:(g2 + 1) * 4, :], in_=pt_[:]), reads=[Bp], writes=[Bws])

            pmA, BpA = PS.get()
            for s_ in range(8):
                wm, Bw = WM.get()
                S.dma("pool", wm[:], w_mod[:, s_ * 512:(s_ + 1) * 512].rearrange("(k p) n -> p k n", p=128), writes=[Bw])
                for mi in range(4):
                    mo = 4 * s_ + mi
                    for k in range(NK):
                        S.op("pe", lambda e: e.matmul(pmA[:, 2 * mo:2 * mo + 2], lhsT=wm[:, k, mi * 128:(mi + 1) * 128], rhs=sT[:, k, :],
                                                      start=(k == 0), stop=(k == NK - 1)),
                             reads=[Bw, BsT], writes=[BpA], inc=(k == NK - 1))
            S.op("dve", lambda e: e.tensor_tensor(out=modp[:, 0:32, :], in0=pmA[:, 0:64].rearrange("p (m c) -> p m c", c=2),
                                                  in1=fmB[:, 0:32].unsqueeze(2).to_broadcast([128, 32, 2]), op=ALU.add),
                 reads=[BpA, Bfm], writes=[Bmod])
            S.op("dve", lambda e: e.scalar_tensor_tensor(out=gm1[:], in0=modp[:, 16:32, 0], scalar=1.0, in1=fmA[:, 32:48], op0=ALU.add, op1=ALU.mult), reads=[Bmod, Bfm], writes=[Bg])
            S.op("dve", lambda e: e.scalar_tensor_tensor(out=cgm1[:], in0=modp[:, 16:32, 1], scalar=1.0, in1=fmA[:, 32:48], op0=ALU.add, op1=ALU.mult), reads=[Bmod, Bfm], writes=[Bg])

        def norm_cast(srcs, Bsrcs, xb, Bxb, junk, Bjunk):
            nt = len(srcs)
            for t in range(nt):
                S.op("act", lambda e: e.activation(out=junk[:], in_=srcs[t], func=AF.Square, accum_out=ss[:, t:t + 1]),
                     reads=[Bsrcs[t]], writes=[Bjunk, Bss])
            S.op("act", lambda e: e.activation(out=ss[:, 0:nt], in_=ss[:, 0:nt], func=AF.Sqrt, scale=1.0 / D, bias=cst[:, 0:1]),
                 reads=[Bss, Bc], writes=[Bss])
            S.op("dve", lambda e: e.reciprocal(out=rstd[:, 0:nt], in_=ss[:, 0:nt]), reads=[Bss], writes=[Brstd])
            for t in range(nt):
                S.op("act", lambda e: e.activation(out=xb[:, t, :], in_=srcs[t], func=AF.Copy, scale=rstd[:, t:t + 1]),
                     reads=[Bsrcs[t], Brstd], writes=[Bxb])

        def trans_mod(xb, Bxb, nt, dst_fn, Bdst, gm, shcol, ctr=[0]):
            for k in range(NK):
                pt_, Bp = PT.get()
                for t in range(nt):
                    S.op("pe", lambda e: e.transpose(out=pt_[:, t, :], in_=xb[:, t, k * 128:(k + 1) * 128], identity=identb[:]),
                         reads=[Bxb, Bc], writes=[Bp], inc=(t == nt - 1))
                dst = dst_fn(k)
                src = pt_[:, 0:nt, :].rearrange("p t n -> p (t n)")
                ctr[0] += 1
                if ctr[0] % 2 == 0:
                    S.op("act", lambda e: e.activation(out=dst, in_=src, func=AF.Identity, scale=gm[:, k:k + 1], bias=shcol(k)),
                         reads=[Bp, Bg, Bmod], writes=[Bdst])
                else:
                    S.op("dve", lambda e: e.tensor_scalar(out=dst, in0=src, scalar1=gm[:, k:k + 1], scalar2=shcol(k), op0=ALU.mult, op1=ALU.add),
                         reads=[Bp, Bg, Bmod], writes=[Bdst])

        if True:
            if True:
                xts = [(sb(p1, f"xt{i}", [128, D]), Buf(f"xt{i}")) for i in range(4)]
                XT = Rot(xts)
                xbs = [(sb(p1, f"xb{i}", [128, 4, D], BF16), Buf(f"xb{i}")) for i in range(2)]
                XB = Rot(xbs)
                junk = sb(p1, "junk1", [128, D], BF16); Bjunk = Buf("junk1")
                for blk in range(5):
                    nt = 2 if blk == 0 else 4
                    srcs, Bs = [], []
                    for t in range(nt):
                        xt, Bx = XT.get()
                        rows = ctx[t * 128:(t + 1) * 128, :] if blk == 0 else x[(blk - 1) * 512 + t * 128:(blk - 1) * 512 + (t + 1) * 128, :]
                        S.dma("sp", xt[:], rows, writes=[Bx])
                        srcs.append(xt[:]); Bs.append(Bx)
                    xb, Bxb = XB.get()
                    norm_cast(srcs, Bs, xb, Bxb, junk, Bjunk)
                    base = 0 if blk == 0 else LC + (blk - 1) * 512
                    gm = cgm1 if blk == 0 else gm1
                    col = 1 if blk == 0 else 0
                    trans_mod(xb, Bxb, nt, lambda k: hT[:, k, base:base + nt * 128], BhT, gm, lambda k: modp[:, k, col:col + 1])
                S.dma("sp", hT_d, hT[:, :, LC:LK], reads=[BhT])
                S.barrier()
                p1.__exit__(None, None, None)

            if stop_after >= 2:
                PS = Rot(pbanks[0:4])
                (oA, BoA), (oB, BoB) = pbanks[4], pbanks[5]
                rc = sb(p2, "rc", [128, L]); rs_ = sb(p2, "rs_", [128, L]); Brope = Buf("rope")
                tdiff = sb(p2, "tdiff", [128, TW]); Btd = Buf("tdiff")
                Dtab = sb(p2, "Dtab", [128, TW]); BD = Buf("Dtab")
                wsl = [(sb(p2, f"wsl{i}", [128, NK, 256], BF16), Buf(f"wsl{i}")) for i in range(3)]
                WS = Rot(wsl)
                kT = sb(p2, "kT", [128, 2, LK], BF16); BkT = Buf("kT")
                vh = sb(p2, "vh", [128, 18, 256], BF16); Bvh = Buf("vh")
                qT = sb(p2, "qT", [128, 2, L], BF16); BqT = Buf("qT")
                gsT = sb(p2, "gsT", [128, 2, L], BF16); BgsT = Buf("gsT")
                rtm = sb(p2, "rtm", [128, 4, 512]); Brtm = Buf("rtm")
                sTs = [(sb(p2, f"sTt{i}", [128, 512], BF16), Buf(f"sTt{i}")) for i in range(3)]
                ST = Rot(sTs)
                sqs = [(sb(p2, f"sq{i}", [128, 512], BF16), Buf(f"sq{i}")) for i in range(2)]
                rsd = sb(p2, "rsd", [128, 512]); Brsd = Buf("rsd")
                rts = [(sb(p2, f"rt{i}", [128, 2, 512], BF16), Buf(f"rt{i}")) for i in range(2)]
                RT = Rot(rts)
                BretD = Buf("retT_d")

                S.dma("sp", rc[:], ropec, writes=[Brope])
                S.dma("sp", rs_[:], ropes, writes=[Brope])
                S.dma("sp", tdiff[:], tdiffd, writes=[Btd])

                def load_rope_slab(off):
                    w, Bw = WS.get()
                    for a in range(2):
                        for b in range(2):
                            S.dma("pool", w[:, :, b * 128 + a * 64: b * 128 + a * 64 + 64],
                                  w_in[:, off + a * 128 + b * 64: off + a * 128 + b * 64 + 64].rearrange("(k p) n -> p k n", p=128),
                                  writes=[Bw])
                    return w, Bw

                def load_slab_from(wsrc, off):
                    w, Bw = WS.get()
                    S.dma("pool", w[:], wsrc[:, off:off + 256].rearrange("(k p) n -> p k n", p=128), writes=[Bw])
                    return w, Bw

                def load_slab(off):
                    return load_slab_from(w_in, off)

                def proj_fm(w, Bw, c, t0, n):
                    ps, Bp = PS.get()
                    for k in range(NK):
                        S.op("pe", lambda e: e.matmul(ps[:, 0:n], lhsT=w[:, k, c * 128:(c + 1) * 128], rhs=hT[:, k, t0:t0 + n],
                                                      start=(k == 0), stop=(k == NK - 1)),
                             reads=[Bw, BhT], writes=[Bp], inc=(k == NK - 1))
                    return ps, Bp

                def rope_evac(pa, Bpa, pb, Bpb, dst, Bdst, tok0, col0):
                    cs = rc[:, tok0:tok0 + 512]; sn = rs_[:, tok0:tok0 + 512]
                    S.op("dve", lambda e: e.tensor_tensor(out=rtm[:, 0, :], in0=pa[:], in1=cs, op=ALU.mult), reads=[Bpa, Brope], writes=[Brtm])
                    S.op("dve", lambda e: e.tensor_tensor(out=rtm[:, 1, :], in0=pb[:], in1=sn, op=ALU.mult), reads=[Bpb, Brope], writes=[Brtm])
                    S.op("dve", lambda e: e.tensor_tensor(out=rtm[:, 2, :], in0=pb[:], in1=cs, op=ALU.mult), reads=[Bpb, Brope], writes=[Brtm])
                    S.op("dve", lambda e: e.tensor_tensor(out=rtm[:, 3, :], in0=pa[:], in1=sn, op=ALU.mult), reads=[Bpa, Brope], writes=[Brtm])
                    S.op("dve", lambda e: e.tensor_tensor(out=dst[:, 0, col0:col0 + 512], in0=rtm[:, 0, :], in1=rtm[:, 1, :], op=ALU.subtract), reads=[Brtm], writes=[Bdst])
                    S.op("dve", lambda e: e.tensor_tensor(out=dst[:, 1, col0:col0 + 512], in0=rtm[:, 2, :], in1=rtm[:, 3, :], op=ALU.add), reads=[Brtm], writes=[Bdst])

                steps = [(0, -2), (1, -1)] + [(2 + j, j) for j in range(16)] + [(0, 16), (1, 17)]
                nheads = H if stop_after >= 3 else stop_after - 1
                nheads = H
                for h in range(nheads):
                    for pc in range(4):
                        sl = slice(pc * 1120, (pc + 1) * 1120)
                        tmpv = rtm[:].rearrange("p a n -> p (a n)")[:, 0:1120]
                        S.op("dve", lambda e: e.tensor_scalar(out=tmpv, in0=tdiff[:, sl], scalar1=0.0, scalar2=lgf[:, h:h + 1], op0=ALU.max, op1=ALU.mult),
                             reads=[Btd, Blg], writes=[Brtm])
                        S.op("dve", lambda e: e.tensor_scalar(out=Dtab[:, sl], in0=tdiff[:, sl], scalar1=0.0, scalar2=nlgb[:, h:h + 1], op0=ALU.min, op1=ALU.mult),
                             reads=[Btd, Blg], writes=[BD])
                        S.op("dve", lambda e: e.tensor_tensor(out=Dtab[:, sl], in0=Dtab[:, sl], in1=tmpv, op=ALU.add), reads=[BD, Brtm], writes=[BD])
                        S.op("act", lambda e: e.activation(out=Dtab[:, sl], in_=Dtab[:, sl], func=AF.Exp, bias=cst[:, 1:2]), reads=[BD, Bc], writes=[BD])
                    w, Bw = load_rope_slab(Q_OFF + h * 256)
                    for ib in range(4):
                        pa, Bpa = proj_fm(w, Bw, 0, LC + ib * 512, 512)
                        pb, Bpb = proj_fm(w, Bw, 1, LC + ib * 512, 512)
                        rope_evac(pa, Bpa, pb, Bpb, qT, BqT, ib * 512, ib * 512)
                    w, Bw = load_rope_slab(K_OFF + h * 256)
                    for c in range(2):
                        ps, Bp = proj_fm(w, Bw, c, 0, LC)
                        S.op("act", lambda e: e.activation(out=kT[:, c, 0:LC], in_=ps[:, 0:LC], func=AF.Copy), reads=[Bp], writes=[BkT])
                    for ib in range(4):
                        pa, Bpa = proj_fm(w, Bw, 0, LC + ib * 512, 512)
                        pb, Bpb = proj_fm(w, Bw, 1, LC + ib * 512, 512)
                        rope_evac(pa, Bpa, pb, Bpb, kT, BkT, ib * 512, LC + ib * 512)
                    w, Bw = load_slab(V_OFF + h * 256)
                    for tc2 in range(9):
                        ps, Bp = PS.get()
                        for j in range(2):
                            tc = tc2 * 2 + j
                            for k in range(NK):
                                S.op("pe", lambda e: e.matmul(ps[:, j * 256:(j + 1) * 256], lhsT=hT[:, k, tc * 128:(tc + 1) * 128], rhs=w[:, k, :],
                                                              start=(k == 0), stop=(k == NK - 1)),
                                     reads=[Bw, BhT], writes=[Bp], inc=(k == NK - 1 and j == 1))
                        dstv = vh[:, tc2 * 2:tc2 * 2 + 2, :].rearrange("p t n -> p (t n)")
                        if tc2 % 2 == 0:
                            S.op("act", lambda e: e.activation(out=dstv, in_=ps[:], func=AF.Copy), reads=[Bp], writes=[Bvh])
                        else:
                            S.op("dve", lambda e: e.tensor_copy(out=dstv, in_=ps[:]), reads=[Bp], writes=[Bvh])
                    w, Bw = load_slab(G_OFF + h * 256)
                    for ib in range(4):
                        for c in range(2):
                            ps, Bp = proj_fm(w, Bw, c, LC + ib * 512, 512)
                            S.op("act", lambda e: e.activation(out=gsT[:, c, ib * 512:(ib + 1) * 512], in_=ps[:], func=AF.Silu), reads=[Bp], writes=[BgsT])
                    for ib in range(4):
                        pend = []

                        def scores(si):
                            kc, jc = steps[si]
                            ps, Bp = PS.get()
                            for c in range(2):
                                S.op("pe", lambda e: e.matmul(ps[:], lhsT=kT[:, c, kc * 128:(kc + 1) * 128], rhs=qT[:, c, ib * 512:(ib + 1) * 512],
                                                              start=(c == 0), stop=(c == 1)),
                                     reads=[BkT, BqT], writes=[Bp], inc=(c == 1))
                            return ps, Bp

                        pend.append(scores(0)); pend.append(scores(1))
                        for si in range(20):
                            kc, jc = steps[si]
                            ps, Bp = pend.pop(0)
                            ws_ = 512 * ib - 128 * jc + TOFF
                            st_, Bst = ST.get()
                            S.op("dve", lambda e: e.tensor_tensor(out=st_[:], in0=ps[:], in1=Dtab[:, ws_:ws_ + 512], op=ALU.mult),
                                 reads=[Bp, BD], writes=[Bst])
                            if si + 2 < 20:
                                pend.append(scores(si + 2))
                            S.op("pe", lambda e: e.matmul(oA[:], lhsT=vh[:, kc, 0:128], rhs=st_[:], start=(si == 0), stop=(si == 19)),
                                 reads=[Bvh, Bst], writes=[BoA], inc=False)
                            S.op("pe", lambda e: e.matmul(oB[:], lhsT=vh[:, kc, 128:256], rhs=st_[:], start=(si == 0), stop=(si == 19)),
                                 reads=[Bvh, Bst], writes=[BoB], inc=True)
                        for (o_, Bo), (sq, Bsq) in zip(((oA, BoA), (oB, BoB)), sqs):
                            S.op("act", lambda e: e.activation(out=sq[:], in_=o_[:], func=AF.Square), reads=[Bo], writes=[Bsq])
                        pss, Bpss = PS.get()
                        S.op("pe", lambda e: e.matmul(pss[:], lhsT=onesb[:], rhs=sqs[0][0][:], start=True, stop=False), reads=[Bc, sqs[0][1]], writes=[Bpss], inc=False)
                        S.op("pe", lambda e: e.matmul(pss[:], lhsT=onesb[:], rhs=sqs[1][0][:], start=False, stop=True), reads=[Bc, sqs[1][1]], writes=[Bpss], inc=True)
                        S.op("act", lambda e: e.activation(out=rsd[:], in_=pss[:], func=AF.Sqrt, scale=1.0 / 256.0, bias=cst[:, 0:1]), reads=[Bpss, Bc], writes=[Brsd])
                        S.op("dve", lambda e: e.reciprocal(out=rsd[:], in_=rsd[:]), reads=[Brsd], writes=[Brsd])
                        rt, Brt = RT.get()
                        for c, (o_, Bo) in enumerate(((oA, BoA), (oB, BoB))):
                            S.op("dve", lambda e: e.tensor_tensor(out=rtm[:, c, :], in0=o_[:], in1=rsd[:], op=ALU.mult), reads=[Bo, Brsd], writes=[Brtm])
                            S.op("dve", lambda e: e.tensor_tensor(out=rt[:, c, :], in0=rtm[:, c, :], in1=gsT[:, c, ib * 512:(ib + 1) * 512], op=ALU.mult),
                                 reads=[Brtm, BgsT], writes=[Brt])
                        S.dma("sp", retT_d[:, 2 * h:2 * h + 2, ib * 512:(ib + 1) * 512], rt[:], reads=[Brt])
                        sidx = 4 * h + ib
                        wmb, Bwmb = load_slab_from(w_mod, 4096 + sidx * 256)
                        pmb, Bpmb = PS.get()
                        for mi in range(2):
                            for k in range(NK):
                                S.op("pe", lambda e: e.matmul(pmb[:, 2 * mi:2 * mi + 2], lhsT=wmb[:, k, mi * 128:(mi + 1) * 128], rhs=sT[:, k, :],
                                                              start=(k == 0), stop=(k == NK - 1)),
                                     reads=[Bwmb, BsT], writes=[Bpmb], inc=(k == NK - 1))
                        m0 = 32 + 2 * sidx
                        S.op("dve", lambda e: e.tensor_tensor(out=modp[:, m0:m0 + 2, :], in0=pmb[:, 0:4].rearrange("p (m c) -> p m c", c=2),
                                                              in1=fmB[:, m0:m0 + 2].unsqueeze(2).to_broadcast([128, 2, 2]), op=ALU.add),
                             reads=[Bpmb, Bfm], writes=[Bmod])
                S.barrier()

        p2.close()
        if stop_after >= 3:
            with ExitStack() as p3:
                PS = Rot(pbanks)
                xres = sb(p3, "xres", [128, 4, D]); Bx = [Buf(f"xres{t}") for t in range(4)]
                hblk = sb(p3, "hblk", [128, NK, 512], BF16); Bhb = Buf("hblk")
                r32 = sb(p3, "r32", [128, 8192]); Br32 = Buf("r32")
                r48 = sb(p3, "r48", [128, 48, 512], BF16); Br48 = [Buf("retblk"), Buf("vn"), Buf("sgT")]
                fngb = sb(p3, "fngb", [128, D]); Bfng = Buf("fngb")
                rsb = sb(p3, "rsb", [128, H, 128]); bsb = sb(p3, "bsb", [128, H, 128])
                slabs = [(sb(p3, f"slab{i}", [128, 4096], BF16), Buf(f"slab{i}")) for i in range(6)]
                SL = Rot(slabs)
                f5 = [(sb(p3, f"f5{i}", [128, 512]), Buf(f"f5{i}")) for i in range(4)]
                F5 = Rot(f5)
                addt = sb(p3, "addt", [128, 128]); Badd = Buf("addt")
                bst = sb(p3, "bst", [128, 4, 8, 6]); Bbst = Buf("bst")
                mv = sb(p3, "mv", [128, 4, 2]); Bmv = Buf("mv")
                nb = sb(p3, "nb", [128, 4]); Bnb = Buf("nb")
                retblk = r48[:, 0:16, :]
                vn = r48[:, 16:32, :].rearrange("p a n -> p (a n)").rearrange("p (t n) -> p t n", t=4)
                sgT = r48[:, 32:48, :]
                aT = r48
                vsf = r32[:].rearrange("p (t n) -> p t n", t=4)
                mrgv = r32[:].bitcast(BF16)[:, 0:8192].rearrange("p (k n) -> p k n", n=512)
                xb2 = r32[:].bitcast(BF16)[:, 0:8192].rearrange("p (t n) -> p t n", t=4)
                junk2 = r32[:].bitcast(BF16)[:, 8192:8192 + D]

                S.op("dve", lambda e: e.scalar_tensor_tensor(out=gm2[:], in0=modp[:, 64:80, 0], scalar=1.0, in1=fmA[:, 48:64], op0=ALU.add, op1=ALU.mult), reads=[Bmod, Bfm], writes=[Bg])
                S.dma("sp", fngb[:], fng.partition_broadcast(128), writes=[Bfng])
                S.dma("sp", bsb[:].rearrange("p g i -> p (g i)"), sgb.partition_broadcast(128), writes=[Brs])
                for g2 in range(2):
                    pr, Bp = PS.get()
                    S.op("pe", lambda e: e.matmul(pr[:], lhsT=onesb[:], rhs=wsT[:, g2 * 4:(g2 + 1) * 4, :].rearrange("p g i -> p (g i)"), start=True, stop=True),
                         reads=[Bc, Bws], writes=[Bp])
                    S.op("dve", lambda e: e.tensor_copy(out=rsb[:, g2 * 4:(g2 + 1) * 4, :].rearrange("p g i -> p (g i)"), in_=pr[:]), reads=[Bp], writes=[Brs])

                def load_w(src, nk, n):
                    sl, Bsl = SL.get()
                    v = sl[:, 0:nk * n].rearrange("p (k n) -> p k n", n=n)
                    S.dma("pool", v, src.rearrange("(k p) n -> p k n", p=128), writes=[Bsl])
                    return v, Bsl

                def fm_group(lhs_fn, rhs_fn, nk, reads, ps=None, Bp=None, n=512, first=True, last=True, k0=0, ktot=None):
                    if ps is None:
                        ps, Bp = PS.get()
                    ktot = nk if ktot is None else ktot
                    for k in range(nk):
                        kk = k0 + k
                        S.op("pe", lambda e: e.matmul(ps[:, 0:n], lhsT=lhs_fn(k), rhs=rhs_fn(k), start=(kk == 0), stop=(kk == ktot - 1)),
                             reads=reads, writes=[Bp], inc=(k == nk - 1))
                    return ps, Bp

                def gated_residual(ps, Bp, gcol, m):
                    mg, Bmg = F5.get()
                    S.op("act", lambda e: e.activation(out=mg[:], in_=ps[:], func=AF.Copy, scale=gcol), reads=[Bp, Bmod], writes=[Bmg])
                    ptr, Bptr = PS.get()
                    for t in range(4):
                        S.op("pe", lambda e: e.transpose(out=ptr[:, t * 128:(t + 1) * 128], in_=mg[:, t * 128:(t + 1) * 128], identity=identf[:]),
                             reads=[Bmg, Bc], writes=[Bptr], inc=(t == 3))
                    S.op("dve", lambda e: e.tensor_tensor(out=xres[:, :, m * 128:(m + 1) * 128], in0=xres[:, :, m * 128:(m + 1) * 128],
                                                          in1=ptr[:].rearrange("p (t n) -> p t n", t=4), op=ALU.add),
                         reads=[Bptr] + Bx, writes=Bx)

                nblk = 4
                for blk in range(nblk):
                    t0 = blk * 512
                    for t in range(4):
                        S.dma("sp", xres[:, t, :], x[t0 + t * 128:t0 + (t + 1) * 128, :], writes=[Bx[t]])
                    if blk == 0 or stop_after < 4:
                        S.dma("sp", hblk[:], hT_d[:, :, t0:t0 + 512], writes=[Bhb])
                    S.dma("sp", retblk, retT_d[:, :, t0:t0 + 512], writes=[Br48[0]])

                    for cg in range(8):
                        wv_, Bsl = load_w(w_in[:, VS_OFF + cg * 256:VS_OFF + (cg + 1) * 256], NK, 256)
                        for t in range(4):
                            ps, Bp = fm_group(lambda k: hblk[:, k, t * 128:(t + 1) * 128], lambda k: wv_[:, k, :], NK, [Bhb, Bsl], n=256)
                            S.op("act", lambda e: e.activation(out=vsf[:, t, cg * 256:(cg + 1) * 256], in_=ps[:, 0:256], func=AF.Gelu_apprx_tanh),
                                 reads=[Bp], writes=[Br32])
                            S.op("dve", lambda e: e.bn_stats(out=bst[:, t, cg, :], in_=vsf[:, t, cg * 256:(cg + 1) * 256]), reads=[Br32], writes=[Bbst])
                    for t in range(4):
                        S.op("dve", lambda e: e.bn_aggr(out=mv[:, t, :], in_=bst[:, t, :, :].rearrange("p a b -> p (a b)")), reads=[Bbst], writes=[Bmv])
                    S.op("act", lambda e: e.activation(out=ss[:, 0:4], in_=mv[:, :, 1], func=AF.Sqrt, bias=cst[:, 0:1]), reads=[Bmv, Bc], writes=[Bss])
                    S.op("dve", lambda e: e.reciprocal(out=rstd[:, 0:4], in_=ss[:, 0:4]), reads=[Bss], writes=[Brstd])
                    S.op("dve", lambda e: e.scalar_tensor_tensor(out=nb[:], in0=mv[:, :, 0], scalar=-1.0, in1=rstd[:, 0:4], op0=ALU.mult, op1=ALU.mult),
                         reads=[Bmv, Brstd], writes=[Bnb])
                    for t in range(4):
                        S.op("act", lambda e: e.activation(out=vn[:, t, :], in_=vsf[:, t, :], func=AF.Identity, scale=rstd[:, t:t + 1], bias=nb[:, t:t + 1]),
                             reads=[Br32, Brstd, Bnb], writes=[Br48[1]])

                    for mp in range(8):
                        wu, Bsl = load_w(w_in[:, U_OFF + mp * 256:U_OFF + (mp + 1) * 256], NK, 256)
                        for mi in range(2):
                            m = 2 * mp + mi
                            pu, Bpu = fm_group(lambda k: wu[:, k, mi * 128:(mi + 1) * 128], lambda k: hblk[:, k, :], NK, [Bsl, Bhb])
                            S.op("act", lambda e: e.activation(out=sgT[:, m, :], in_=pu[:], func=AF.Gelu_apprx_tanh), reads=[Bpu], writes=[Br48[2]])
                    for m in range(NK):
                        g = m // 2
                        pm_, Bpm = PS.get()
                        for t in range(4):
                            S.op("pe", lambda e: e.matmul(pm_[:, t * 128:(t + 1) * 128], lhsT=vn[:, t, m * 128:(m + 1) * 128], rhs=wsT[:, g, :], start=True, stop=True),
                                 reads=[Br48[1], Bws], writes=[Bpm], inc=(t == 3))
                        S.op("dve", lambda e: e.scalar_tensor_tensor(out=addt[:], in0=rsb[:, g, :], scalar=fmA[:, 80 + m:81 + m], in1=bsb[:, g, :], op0=ALU.mult, op1=ALU.add),
                             reads=[Brs, Bfm], writes=[Badd])
                        tm, Btm = F5.get()
                        S.op("dve", lambda e: e.scalar_tensor_tensor(out=tm[:].rearrange("p (t n) -> p t n", t=4), in0=pm_[:].rearrange("p (t n) -> p t n", t=4),
                                                                     scalar=fmA[:, 64 + m:65 + m], in1=addt[:].unsqueeze(1).to_broadcast([128, 4, 128]),
                                                                     op0=ALU.mult, op1=ALU.add),
                             reads=[Bpm, Bfm, Badd], writes=[Btm])
                        S.op("dve", lambda e: e.tensor_tensor(out=sgT[:, m, :], in0=tm[:], in1=sgT[:, m, :], op=ALU.mult), reads=[Btm, Br48[2]], writes=[Br48[2]])

                    for mp in range(8):
                        cs_ = slice(mp * 256, (mp + 1) * 256)
                        wso, Bwso = load_w(w_sg_o[:, cs_], NK, 256)
                        wgs, Bwgs = load_w(w_in[:, GS_OFF + mp * 256:GS_OFF + (mp + 1) * 256], NK, 256)
                        wro, Bwro = load_w(w_ret_o[:, cs_], NK, 256)
                        wgr, Bwgr = load_w(w_in[:, GR_OFF + mp * 256:GR_OFF + (mp + 1) * 256], NK, 256)
                        msl = [slice(0, 128), slice(128, 256)]
                        p1 = [fm_group(lambda k: wso[:, k, msl[mi]], lambda k: sgT[:, k, :], NK, [Bwso, Br48[2]]) for mi in range(2)]
                        p2 = [fm_group(lambda k: wgs[:, k, msl[mi]], lambda k: hblk[:, k, :], NK, [Bwgs, Bhb]) for mi in range(2)]
                        Pts = []
                        for mi in range(2):
                            gsf, Bgsf = F5.get()
                            S.op("act", lambda e: e.activation(out=gsf[:], in_=p2[mi][0][:], func=AF.Sigmoid), reads=[p2[mi][1]], writes=[Bgsf])
                            S.op("dve", lambda e: e.tensor_tensor(out=gsf[:], in0=p1[mi][0][:], in1=gsf[:], op=ALU.mult), reads=[p1[mi][1], Bgsf], writes=[Bgsf])
                            Pts.append((gsf, Bgsf))
                        p3 = [fm_group(lambda k: wro[:, k, msl[mi]], lambda k: retblk[:, k, :], NK, [Bwro, Br48[0]]) for mi in range(2)]
                        p4 = [fm_group(lambda k: wgr[:, k, msl[mi]], lambda k: hblk[:, k, :], NK, [Bwgr, Bhb]) for mi in range(2)]
                        for mi in range(2):
                            m = 2 * mp + mi
                            grf, Bgrf = F5.get()
                            S.op("act", lambda e: e.activation(out=grf[:], in_=p4[mi][0][:], func=AF.Sigmoid), reads=[p4[mi][1]], writes=[Bgrf])
                            S.op("dve", lambda e: e.tensor_tensor(out=grf[:], in0=p3[mi][0][:], in1=grf[:], op=ALU.mult), reads=[p3[mi][1], Bgrf], writes=[Bgrf])
                            S.op("dve", lambda e: e.tensor_tensor(out=mrgv[:, m, :], in0=Pts[mi][0][:], in1=grf[:], op=ALU.add), reads=[Pts[mi][1], Bgrf], writes=[Br32])

                    for mp in range(8):
                        wo, Bsl = load_w(w_out[:, mp * 256:(mp + 1) * 256], NK, 256)
                        for mi in range(2):
                            m = 2 * mp + mi
                            ps, Bp = fm_group(lambda k: wo[:, k, mi * 128:(mi + 1) * 128], lambda k: mrgv[:, k, :], NK, [Bsl, Br32])
                            gated_residual(ps, Bp, modp[:, 32 + m, 0:1], m)

                    if stop_after >= 4:
                        for t in range(4):
                            S.op("act", lambda e: e.activation(out=junk2, in_=xres[:, t, :], func=AF.Square, accum_out=ss[:, t:t + 1]),
                                 reads=[Bx[t]], writes=[Br32, Bss])
                        S.op("act", lambda e: e.activation(out=ss[:, 0:4], in_=ss[:, 0:4], func=AF.Sqrt, scale=1.0 / D, bias=cst[:, 0:1]), reads=[Bss, Bc], writes=[Bss])
                        S.op("dve", lambda e: e.reciprocal(out=rstd[:, 0:4], in_=ss[:, 0:4]), reads=[Bss], writes=[Brstd])
                        for t in range(4):
                            S.op("act", lambda e: e.activation(out=xb2[:, t, :], in_=xres[:, t, :], func=AF.Copy, scale=rstd[:, t:t + 1]),
                                 reads=[Bx[t], Brstd], writes=[Br32])
                        trans_mod(xb2, Br32, 4, lambda k: hblk[:, k, :], Bhb, gm2, lambda k: modp[:, 48 + k, 0:1])
                        for cp in range(NFC // 2):
                            wa, Bwa = load_w(w_ffn_in[:, cp * 256:(cp + 1) * 256], NK, 256)
                            wb, Bwb = load_w(w_ffn_in[:, FH + cp * 256:FH + (cp + 1) * 256], NK, 256)
                            for ci in range(2):
                                cc = 2 * cp + ci
                                csl = slice(ci * 128, (ci + 1) * 128)
                                pa, Bpa = fm_group(lambda k: wa[:, k, csl], lambda k: hblk[:, k, :], NK, [Bwa, Bhb])
                                pb, Bpb = fm_group(lambda k: wb[:, k, csl], lambda k: hblk[:, k, :], NK, [Bwb, Bhb])
                                sa, Bsa = F5.get()
                                S.op("act", lambda e: e.activation(out=sa[:], in_=pa[:], func=AF.Silu), reads=[Bpa], writes=[Bsa])
                                S.op("dve", lambda e: e.tensor_tensor(out=aT[:, cc, :], in0=sa[:], in1=pb[:], op=ALU.mult), reads=[Bsa, Bpb], writes=Br48)
                        if blk + 1 < nblk:
                            S.dma("sp", hblk[:], hT_d[:, :, t0 + 512:t0 + 1024], writes=[Bhb])
                        for m in range(NK):
                            ps, Bp = PS.get()
                            for hf in range(2):
                                wf, Bsl = load_w(w_ffn_out[hf * 22 * 128:(hf + 1) * 22 * 128, m * 128:(m + 1) * 128], 22, 128)
                                fm_group(lambda k: wf[:, k, :], lambda k: aT[:, hf * 22 + k, :], 22, [Bsl] + Br48, ps=ps, Bp=Bp, k0=hf * 22, ktot=NFC)
                            gated_residual(ps, Bp, modp[:, 80 + m, 0:1], m)
                        for t in range(4):
                            S.op("act", lambda e: e.activation(out=junk2, in_=xres[:, t, :], func=AF.Square, accum_out=ss[:, t:t + 1]),
                                 reads=[Bx[t]], writes=[Br32, Bss])
                        S.op("act", lambda e: e.activation(out=ss[:, 0:4], in_=ss[:, 0:4], func=AF.Sqrt, scale=1.0 / D, bias=cst[:, 0:1]), reads=[Bss, Bc], writes=[Bss])
                        S.op("dve", lambda e: e.reciprocal(out=rstd[:, 0:4], in_=ss[:, 0:4]), reads=[Bss], writes=[Brstd])
                        for t in range(4):
                            S.op("dve", lambda e: e.scalar_tensor_tensor(out=xres[:, t, :], in0=xres[:, t, :], scalar=rstd[:, t:t + 1], in1=fngb[:], op0=ALU.mult, op1=ALU.mult),
                                 reads=[Bx[t], Brstd, Bfng], writes=[Bx[t]])
                    for t in range(4):
                        S.dma("sp", out[t0 + t * 128:t0 + (t + 1) * 128, :], xres[:, t, :], reads=[Bx[t]])
                S.barrier()
        S.barrier()
        print("program: ops", S.nops, "waits", S.nwaits)
    return nc


def _consts():
    ident = np.eye(128, dtype=np.float32)
    freq = (10000.0 ** (-np.arange(64, dtype=np.float32) / np.float32(64))).astype(np.float32)
    tok = np.arange(L)
    row = (tok // 64).astype(np.float32)
    col = (tok % 64).astype(np.float32)
    ang = np.concatenate([row[None, :] * freq[:, None], col[None, :] * freq[:, None]], axis=0).astype(np.float32)
    rc = np.cos(ang.astype(np.float64)).astype(np.float32)
    rs = np.sin(ang.astype(np.float64)).astype(np.float32)
    m = np.arange(TW, dtype=np.float32)[None, :]
    p = np.arange(128, dtype=np.float32)[:, None]
    td = (m - p - TOFF).astype(np.float32)
    return ident, rc, rs, td


_PROG = {}


def kernel(x, c, ctx, c_ctx, w_mod, b_mod, norm1_g, w_in, ret_decay_fwd, ret_decay_bwd,
           sg_ln_g, sg_ln_b, sg_w, sg_b, w_ret_o, w_sg_o, w_out, norm2_g,
           w_ffn_in, w_ffn_out, final_norm_g, _stop_after=99, _cores=8):
    f = lambda a: np.ascontiguousarray(np.asarray(a, dtype=np.float32))
    ident, rc, rs, td = _consts()
    if _stop_after not in _PROG:
        _PROG[_stop_after] = build_program(_stop_after)
    nc = _PROG[_stop_after]
    shared = {
        "c_ctx": f(c_ctx).reshape(NK, 128), "w_mod": f(w_mod)[0], "b_mod": f(b_mod)[0].reshape(96, 128),
        "norm1_g": f(norm1_g)[0].reshape(NK, 128), "w_in": f(w_in)[0],
        "ret_decay_fwd": f(ret_decay_fwd).reshape(1, H), "ret_decay_bwd": f(ret_decay_bwd).reshape(1, H),
        "sg_ln_g": f(sg_ln_g)[0].reshape(NK, 128), "sg_ln_b": f(sg_ln_b)[0].reshape(NK, 128),
        "sg_w": f(sg_w)[0], "sg_b": f(sg_b)[0].reshape(1, H * 128),
        "w_ret_o": f(w_ret_o)[0], "w_sg_o": f(w_sg_o)[0], "w_out": f(w_out)[0],
        "norm2_g": f(norm2_g)[0].reshape(NK, 128), "w_ffn_in": f(w_ffn_in)[0], "w_ffn_out": f(w_ffn_out)[0],
        "final_norm_g": f(final_norm_g).reshape(1, D),
        "k_ident": ident, "k_rope_c": rc, "k_rope_s": rs, "k_tdiff": td,
    }
    xs, cs, cxs = f(x), f(c), f(ctx)
    in_maps = []
    for b in range(_cores):
        m = dict(shared)
        m["x"] = xs[b]; m["c"] = cs[b].reshape(NK, 128); m["ctx"] = cxs[b]
        in_maps.append(m)
    res = run_bass_kernel_spmd(nc, in_maps, core_ids=list(range(_cores)))
    if _stop_after < 99:
        return res
    return np.stack([np.asarray(r["out"], dtype=np.float32) for r in res.results], axis=0)
```
